# Optimizing a Trainium2 kernel written in Bass

```python
import math
import jax, jax.numpy as jnp
from jax import lax
import numpy as np

D_MODEL = 2048
BATCH = 8
SEQ = 4096
DEPTH = 2

GRID_W = 64
CTX_LEN = 256
BLK = 128
ROPE_BASE = 10000.0
EPS = 1e-6

MLA_HEADS = 6
MLA_Q_RANK = 512
MLA_KV_RANK = 256
MLA_NOPE = 128
MLA_ROPE = 64
MLA_V = 128
MLA_QK = MLA_NOPE + MLA_ROPE
MLA_WIDTH = MLA_HEADS * MLA_V

SWA_Q_HEADS = 6
SWA_KV_HEADS = 2
SWA_GROUP = SWA_Q_HEADS // SWA_KV_HEADS
SWA_HEAD_DIM = 128
SWA_WINDOW = 128
SWA_WIDTH = SWA_Q_HEADS * SWA_HEAD_DIM

DIFF_HEADS = 4
DIFF_QK_DIM = 64
DIFF_V_DIM = 2 * DIFF_QK_DIM
DIFF_WIDTH = DIFF_HEADS * DIFF_V_DIM

MIX_WIDTH = MLA_WIDTH + SWA_WIDTH + DIFF_WIDTH

IN_LAYOUT = (
    ("mla_cq", MLA_Q_RANK), ("mla_ckv", MLA_KV_RANK), ("mla_kr", MLA_ROPE), ("mla_gate", MLA_WIDTH),
    ("swa_q", SWA_WIDTH), ("swa_k", SWA_KV_HEADS * SWA_HEAD_DIM), ("swa_v", SWA_KV_HEADS * SWA_HEAD_DIM), ("swa_gate", SWA_WIDTH),
    ("dif_q", 2 * DIFF_HEADS * DIFF_QK_DIM), ("dif_k", 2 * DIFF_HEADS * DIFF_QK_DIM), ("dif_v", DIFF_WIDTH), ("dif_gate", DIFF_WIDTH),
)
IN_COLS = sum(n for _, n in IN_LAYOUT)

kernel_name = "hybrid_mla_swa_diffattn_prefix_dit"

F32 = jnp.float32


def rms_norm(x, g):
    xf = x.astype(F32)
    y = xf * lax.rsqrt(jnp.mean(xf * xf, axis=-1, keepdims=True) + EPS)
    return (y * g.astype(F32)).astype(x.dtype)


def split_cols(p):
    idx = np.cumsum([n for _, n in IN_LAYOUT])[:-1].tolist()
    parts = jnp.split(p, idx, axis=-1)
    return {name: t for (name, _), t in zip(IN_LAYOUT, parts)}


def adaln(cvec, w, b):
    m = jax.nn.silu(cvec) @ w + b
    return jnp.split(m, 3, axis=-1)


def axial_rope_tables(n_tokens, rot_dim):
    rows = n_tokens // GRID_W
    t_row = jnp.repeat(jnp.arange(rows), GRID_W).astype(F32)
    t_col = jnp.tile(jnp.arange(GRID_W), rows).astype(F32)
    n_freq = rot_dim // 4
    inv = jnp.power(ROPE_BASE, -jnp.arange(n_freq, dtype=F32) / n_freq)
    ang = jnp.concatenate([t_row[:, None] * inv, t_col[:, None] * inv], axis=-1)
    return jnp.cos(ang), jnp.sin(ang)


def apply_rope(x, cos, sin):
    half = x.shape[-1] // 2
    shp = (cos.shape[0],) + (1,) * (x.ndim - 3) + (half,)
    c = cos.reshape(shp).astype(x.dtype)
    s = sin.reshape(shp).astype(x.dtype)
    x1, x2 = x[..., :half], x[..., half:]
    return jnp.concatenate([x1 * c - x2 * s, x1 * s + x2 * c], axis=-1)


def rope_tail(x, cos, sin, r):
    return jnp.concatenate([x[..., :-r], apply_rope(x[..., -r:], cos, sin)], axis=-1)


def block_sweep(fn, q):
    b, s = q.shape[:2]
    nb = s // BLK
    qb = jnp.moveaxis(q.reshape((b, nb, BLK) + q.shape[2:]), 1, 0)
    out = jnp.moveaxis(lax.map(fn, qb), 0, 1)
    return out.reshape((b, s) + out.shape[3:])


def attend(q, k, v, scale):
    s = jnp.einsum('bqhd,bthd->bhqt', q, k).astype(F32) * scale
    p = jax.nn.softmax(s, axis=-1)
    return jnp.einsum('bhqt,bthd->bqhd', p.astype(v.dtype), v)


def mla_queries(cq, q_norm_g, w_uq, q_gain):
    q = (rms_norm(cq, q_norm_g) @ w_uq).reshape(cq.shape[0], cq.shape[1], MLA_HEADS, MLA_QK)
    return rms_norm(q, q_gain)


def mla_keys_values(ckv, kr, kv_norm_g, w_ukv, k_gain):
    kv = (rms_norm(ckv, kv_norm_g) @ w_ukv).reshape(ckv.shape[0], ckv.shape[1], MLA_HEADS, MLA_NOPE + MLA_V)
    k_nope, v = kv[..., :MLA_NOPE], kv[..., MLA_NOPE:]
    k_pe = jnp.broadcast_to(kr[:, :, None, :], k_nope.shape[:3] + (MLA_ROPE,))
    k = rms_norm(jnp.concatenate([k_nope, k_pe], axis=-1), k_gain)
    return k, v


def swa_latent(q, k, v, kc, vc, sink):
    b, s = q.shape[:2]
    nb = s // BLK
    scale = SWA_HEAD_DIM ** -0.5
    qb = q.reshape(b, nb, BLK, SWA_KV_HEADS, SWA_GROUP, SWA_HEAD_DIM)

    def band(t):
        tp = jnp.pad(t, ((0, 0), (BLK, BLK), (0, 0), (0, 0))).reshape(b, nb + 2, BLK, SWA_KV_HEADS, SWA_HEAD_DIM)
        return jnp.concatenate([tp[:, :-2], tp[:, 1:-1], tp[:, 2:]], axis=2)

    kw, vw = band(k), band(v)
    rel = jnp.arange(3 * BLK)[None, :] - BLK - jnp.arange(BLK)[:, None]
    kpos = jnp.arange(nb)[:, None] * BLK + jnp.arange(3 * BLK)[None, :] - BLK
    mask = (jnp.abs(rel) <= SWA_WINDOW)[None] & ((kpos >= 0) & (kpos < s))[:, None, :]

    s_w = jnp.einsum('bnqhgd,bnkhd->bnhgqk', qb, kw).astype(F32) * scale
    s_w = jnp.where(mask[None, :, None, None], s_w, -jnp.inf)
    s_c = jnp.einsum('bnqhgd,bchd->bnhgqc', qb, kc).astype(F32) * scale
    s_k = jnp.broadcast_to(sink.reshape(SWA_KV_HEADS, SWA_GROUP, 1, 1).astype(F32), s_c.shape[:-1] + (1,))
    p = jax.nn.softmax(jnp.concatenate([s_w, s_c, s_k], axis=-1), axis=-1)
    n_w, n_c = 3 * BLK, kc.shape[1]
    p_w = p[..., :n_w].astype(v.dtype)
    p_c = p[..., n_w:n_w + n_c].astype(v.dtype)
    o = jnp.einsum('bnhgqk,bnkhd->bnqhgd', p_w, vw) + jnp.einsum('bnhgqc,bchd->bnqhgd', p_c, vc)
    return o.reshape(b, s, SWA_WIDTH)


def swa_context(q, k, v, sink):
    b, lc = q.shape[:2]
    s = jnp.einsum('bqhgd,bkhd->bhgqk', q, k).astype(F32) * (SWA_HEAD_DIM ** -0.5)
    s_k = jnp.broadcast_to(sink.reshape(SWA_KV_HEADS, SWA_GROUP, 1, 1).astype(F32), s.shape[:-1] + (1,))
    p = jax.nn.softmax(jnp.concatenate([s, s_k], axis=-1), axis=-1)[..., :-1]
    o = jnp.einsum('bhgqk,bkhd->bqhgd', p.astype(v.dtype), v)
    return o.reshape(b, lc, SWA_WIDTH)


def diff_attend(q, k, v, lam):
    s = jnp.einsum('bqhcd,bthcd->bhcqt', q, k).astype(F32) * (DIFF_QK_DIM ** -0.5)
    p = jax.nn.softmax(s, axis=-1)
    a = p[:, :, 0] - lam * p[:, :, 1]
    return jnp.einsum('bhqt,bthd->bqhd', a.astype(v.dtype), v)


def setup_inputs(seed: int = 0) -> dict:
    key = jax.random.key(seed)
    ks = jax.random.split(key, 25)
    L = DEPTH

    def nrm(k, shape, scale):
        return jax.random.normal(k, shape, F32) * scale

    def gain(k, shape):
        return 1.0 + 0.01 * jax.random.normal(k, shape, F32)

    return {
        "x": nrm(ks[0], (BATCH, SEQ, D_MODEL), 1.0),
        "c": nrm(ks[1], (BATCH, D_MODEL), 1.0),
        "ctx": nrm(ks[2], (BATCH, CTX_LEN, D_MODEL), 1.0),
        "c_ctx": nrm(ks[3], (D_MODEL,), 1.0),
        "norm_g": gain(ks[4], (L, D_MODEL)),
        "w_ada": nrm(ks[5], (L, D_MODEL, 3 * D_MODEL), 0.5 * D_MODEL ** -0.5),
        "b_ada": nrm(ks[6], (L, 3 * D_MODEL), 0.01),
        "w_in": nrm(ks[7], (L, D_MODEL, IN_COLS), D_MODEL ** -0.5),
        "mla_q_norm": gain(ks[8], (L, MLA_Q_RANK)),
        "mla_w_uq": nrm(ks[9], (L, MLA_Q_RANK, MLA_HEADS * MLA_QK), MLA_Q_RANK ** -0.5),
        "mla_kv_norm": gain(ks[10], (L, MLA_KV_RANK)),
        "mla_w_ukv": nrm(ks[11], (L, MLA_KV_RANK, MLA_HEADS * (MLA_NOPE + MLA_V)), MLA_KV_RANK ** -0.5),
        "mla_q_gain": gain(ks[12], (L, MLA_QK)),
        "mla_k_gain": gain(ks[13], (L, MLA_QK)),
        "swa_q_gain": gain(ks[14], (L, SWA_HEAD_DIM)),
        "swa_k_gain": gain(ks[15], (L, SWA_HEAD_DIM)),
        "swa_sink": nrm(ks[16], (L, SWA_Q_HEADS), 0.5),
        "dif_q_gain": gain(ks[17], (L, DIFF_QK_DIM)),
        "dif_k_gain": gain(ks[18], (L, DIFF_QK_DIM)),
        "dif_lq1": nrm(ks[19], (L, DIFF_QK_DIM), 0.1),
        "dif_lk1": nrm(ks[20], (L, DIFF_QK_DIM), 0.1),
        "dif_lq2": nrm(ks[21], (L, DIFF_QK_DIM), 0.1),
        "dif_lk2": nrm(ks[22], (L, DIFF_QK_DIM), 0.1),
        "dif_out_gain": gain(ks[23], (L, DIFF_V_DIM)),
        "w_out": nrm(ks[24], (L, MIX_WIDTH, D_MODEL), MIX_WIDTH ** -0.5),
    }


def reference(x, c, ctx, c_ctx, norm_g, w_ada, b_ada, w_in, mla_q_norm, mla_w_uq, mla_kv_norm, mla_w_ukv,
              mla_q_gain, mla_k_gain, swa_q_gain, swa_k_gain, swa_sink, dif_q_gain, dif_k_gain,
              dif_lq1, dif_lk1, dif_lq2, dif_lk2, dif_out_gain, w_out):
    b, s, _ = x.shape
    lc = ctx.shape[1]
    cos64, sin64 = axial_rope_tables(s, MLA_ROPE)
    cos128, sin128 = axial_rope_tables(s, SWA_HEAD_DIM)
    hx, hc = x, ctx

    for l in range(DEPTH):
        need_ctx_out = l < DEPTH - 1
        sh_x, sc_x, g_x = [t[:, None, :] for t in adaln(c, w_ada[l], b_ada[l])]
        sh_c, sc_c, g_c = adaln(c_ctx, w_ada[l], b_ada[l])
        nx = rms_norm(hx, norm_g[l]) * (1.0 + sc_x) + sh_x
        nc = rms_norm(hc, norm_g[l]) * (1.0 + sc_c) + sh_c
        px = split_cols(nx @ w_in[l])
        pc = split_cols(nc @ w_in[l])

        qa_x = rope_tail(mla_queries(px["mla_cq"], mla_q_norm[l], mla_w_uq[l], mla_q_gain[l]), cos64, sin64, MLA_ROPE)
        ka_x, va_x = mla_keys_values(px["mla_ckv"], px["mla_kr"], mla_kv_norm[l], mla_w_ukv[l], mla_k_gain[l])
        ka_x = rope_tail(ka_x, cos64, sin64, MLA_ROPE)
        ka_c, va_c = mla_keys_values(pc["mla_ckv"], pc["mla_kr"], mla_kv_norm[l], mla_w_ukv[l], mla_k_gain[l])
        ka_all = jnp.concatenate([ka_x, ka_c], axis=1)
        va_all = jnp.concatenate([va_x, va_c], axis=1)
        ya_x = block_sweep(lambda qb: attend(qb, ka_all, va_all, MLA_QK ** -0.5), qa_x).reshape(b, s, MLA_WIDTH)

        qb_x = apply_rope(rms_norm(px["swa_q"].reshape(b, s, SWA_KV_HEADS, SWA_GROUP, SWA_HEAD_DIM), swa_q_gain[l]), cos128, sin128)
        kb_x = apply_rope(rms_norm(px["swa_k"].reshape(b, s, SWA_KV_HEADS, SWA_HEAD_DIM), swa_k_gain[l]), cos128, sin128)
        vb_x = px["swa_v"].reshape(b, s, SWA_KV_HEADS, SWA_HEAD_DIM)
        kb_c = rms_norm(pc["swa_k"].reshape(b, lc, SWA_KV_HEADS, SWA_HEAD_DIM), swa_k_gain[l])
        vb_c = pc["swa_v"].reshape(b, lc, SWA_KV_HEADS, SWA_HEAD_DIM)
        yb_x = swa_latent(qb_x, kb_x, vb_x, kb_c, vb_c, swa_sink[l])

        lam_init = 0.8 - 0.6 * math.exp(-0.3 * l)
        lam = (jnp.exp(jnp.sum(dif_lq1[l].astype(F32) * dif_lk1[l].astype(F32)))
               - jnp.exp(jnp.sum(dif_lq2[l].astype(F32) * dif_lk2[l].astype(F32))) + lam_init)
        qc_x = apply_rope(rms_norm(px["dif_q"].reshape(b, s, DIFF_HEADS, 2, DIFF_QK_DIM), dif_q_gain[l]), cos64, sin64)
        kc_x = apply_rope(rms_norm(px["dif_k"].reshape(b, s, DIFF_HEADS, 2, DIFF_QK_DIM), dif_k_gain[l]), cos64, sin64)
        vc_x = px["dif_v"].reshape(b, s, DIFF_HEADS, DIFF_V_DIM)
        kc_c = rms_norm(pc["dif_k"].reshape(b, lc, DIFF_HEADS, 2, DIFF_QK_DIM), dif_k_gain[l])
        vc_c = pc["dif_v"].reshape(b, lc, DIFF_HEADS, DIFF_V_DIM)
        kc_all = jnp.concatenate([kc_x, kc_c], axis=1)
        vc_all = jnp.concatenate([vc_x, vc_c], axis=1)
        oc_x = block_sweep(lambda qb: diff_attend(qb, kc_all, vc_all, lam), qc_x)
        yc_x = (rms_norm(oc_x, dif_out_gain[l]) * (1.0 - lam_init)).reshape(b, s, DIFF_WIDTH)

        y_x = jnp.concatenate([ya_x * jax.nn.silu(px["mla_gate"]),
                               yb_x * jax.nn.silu(px["swa_gate"]),
                               yc_x * jax.nn.silu(px["dif_gate"])], axis=-1) @ w_out[l]

        if need_ctx_out:
            qa_c = mla_queries(pc["mla_cq"], mla_q_norm[l], mla_w_uq[l], mla_q_gain[l])
            ya_c = attend(qa_c, ka_c, va_c, MLA_QK ** -0.5).reshape(b, lc, MLA_WIDTH)
            qb_c = rms_norm(pc["swa_q"].reshape(b, lc, SWA_KV_HEADS, SWA_GROUP, SWA_HEAD_DIM), swa_q_gain[l])
            yb_c = swa_context(qb_c, kb_c, vb_c, swa_sink[l])
            qc_c = rms_norm(pc["dif_q"].reshape(b, lc, DIFF_HEADS, 2, DIFF_QK_DIM), dif_q_gain[l])
            yc_c = (rms_norm(diff_attend(qc_c, kc_c, vc_c, lam), dif_out_gain[l]) * (1.0 - lam_init)).reshape(b, lc, DIFF_WIDTH)
            y_c = jnp.concatenate([ya_c * jax.nn.silu(pc["mla_gate"]),
                                   yb_c * jax.nn.silu(pc["swa_gate"]),
                                   yc_c * jax.nn.silu(pc["dif_gate"])], axis=-1) @ w_out[l]
            hc = hc + g_c * y_c

        hx = hx + g_x * y_x

    return hx
```

```python
import math
from contextlib import ExitStack

import numpy as np
import concourse.bass as bass
import concourse.mybir as mybir
from concourse.bass_utils import run_bass_kernel_spmd

F32 = mybir.dt.float32
BF16 = mybir.dt.bfloat16
AF = mybir.ActivationFunctionType
ALU = mybir.AluOpType
AX = mybir.AxisListType

ENGS = ("pe", "act", "dve", "pool", "sp")
EPS = 1e-6
D = 2048
KD = 16
GRID_W = 64
IN_COLS = 5696
GROUPS = [
    ("cq", 0, 512), ("ckv", 512, 320),
    ("mgate0", 832, 384), ("mgate1", 1216, 384),
    ("swaq0", 1600, 384), ("swaq1", 1984, 384),
    ("swakv", 2368, 512),
    ("sgate0", 2880, 384), ("sgate1", 3264, 384),
    ("difq", 3648, 512), ("difk", 4160, 512), ("difv", 4672, 512), ("dgate", 5184, 512),
]


class Prog:
    uid = 0

    sempool = None

    def __init__(self, nc, same_engine_sync=True):
        self.nc = nc
        self.key_queue = {}
        self.same_engine_sync = same_engine_sync
        self.streams = {e: [] for e in ENGS}
        self.last_w = {}
        self.readers = {}
        self.dma_cnt = {}
        self.tok_info = []
        self.needed = set()
        self.sealed = {}

    def _new_tok(self, kind, who, idx):
        self.tok_info.append([kind, who, idx])
        return len(self.tok_info) - 1

    def _deps(self, reads, writes):
        deps = set()
        for r in reads:
            t = self.last_w.get(r)
            if t is not None:
                deps.add(t)
        for w in writes:
            t = self.last_w.get(w)
            if t is not None:
                deps.add(t)
            for t in self.readers.get(w, ()):
                deps.add(t)
        return deps

    def _commit(self, tok, reads, writes):
        for r in reads:
            self.readers.setdefault(r, []).append(tok)
        for w in writes:
            self.last_w[w] = tok
            self.readers[w] = []

    def op(self, eng, fn, reads=(), writes=()):
        reads = tuple(reads); writes = tuple(writes)
        deps = self._deps(reads, writes)
        tok = self._new_tok("eng", eng, len(self.streams[eng]))
        self.streams[eng].append(dict(fn=fn, deps=deps, tok=tok, dma=None))
        self._commit(tok, reads, writes)
        return tok

    def dma(self, queue, out, in_, reads=(), writes=(), key=None, **kw):
        reads = tuple(reads); writes = tuple(writes)
        deps = self._deps(reads, writes)
        c = self.dma_cnt.get(key, 0) + 1
        self.dma_cnt[key] = c
        assert self.key_queue.setdefault(key, queue) == queue
        tok = self._new_tok("dma", key, c)
        self.streams[queue].append(dict(fn=None, deps=deps, tok=tok, dma=(out, in_, key, kw)))
        self._commit(tok, reads, writes)
        return tok

    def seal(self, key):
        c = self.dma_cnt.get(key, 0)
        lo = self.sealed.get(key, 0)
        for t in self.tok_info:
            if t[0] == "dma" and t[1] == key and t[2] > lo:
                t[2] = c
        self.sealed[key] = c

    def emit(self):
        nc = self.nc

        def relevant(eng, d):
            kind, who, idx = self.tok_info[d]
            if kind == "eng" and who == eng:
                if eng in ("pe", "sp"):
                    return False
                return self.same_engine_sync
            return True

        for eng in ENGS:
            for o in self.streams[eng]:
                o["deps"] = sorted(d for d in o["deps"] if relevant(eng, d))
                for d in o["deps"]:
                    self.needed.add(d)
        tok_count = {}
        eng_total = {}
        for eng in ENGS:
            cnt = 0
            for o in self.streams[eng]:
                if o["dma"] is None and o["tok"] in self.needed:
                    cnt += 1
                    tok_count[o["tok"]] = cnt
            eng_total[eng] = cnt
        dma_keys = sorted(self.dma_cnt.keys())
        pool = self.sempool
        with ExitStack() as es:
            esem, ebase = {}, {}
            for e in ENGS:
                h = pool.get("eng", e)
                esem[e] = h[0]; ebase[e] = h
            dsem, dbase = {}, {}
            used = {"hw": 0, "sw": 0}
            for k in dma_keys:
                role = "sw" if self.key_queue[k] == "pool" else "hw"
                h = pool.get(role, used[role])
                used[role] += 1
                dsem[k] = h[0]; dbase[k] = h
            def mk(eng):
                def body(e):
                    waited = {}
                    for o in self.streams[eng]:
                        for d in o["deps"]:
                            kind, who, idx = self.tok_info[d]
                            if kind == "eng":
                                sem, val, wk = esem[who], ebase[who][1] + tok_count[d], ("e", who)
                            else:
                                sem, val, wk = dsem[who], dbase[who][1] + 16 * idx, ("d", who)
                            if waited.get(wk, 0) >= val:
                                continue
                            waited[wk] = val
                            e.wait_ge(sem, val)
                        if o["dma"] is not None:
                            out, in_, key, kw = o["dma"]
                            e.dma_start(out=out, in_=in_, **kw).then_inc(dsem[key], 16)
                        else:
                            ins = o["fn"](e)
                            if o["tok"] in self.needed:
                                ins.then_inc(esem[eng], 1)
                    if eng == "sp":
                        for k in dma_keys:
                            e.wait_ge(dsem[k], dbase[k][1] + 16 * self.dma_cnt[k])
                return body

            with nc.Block() as block:
                handles = {"pe": block.tensor, "act": block.scalar, "dve": block.vector,
                           "pool": block.gpsimd, "sp": block.sync}
                for eng in ENGS:
                    handles[eng](mk(eng))
            for e in ENGS:
                ebase[e][1] += eng_total[e]
            for k in dma_keys:
                dbase[k][1] += 16 * self.dma_cnt[k]


class SemPool:
    def __init__(self, nc, es):
        self.nc, self.es, self.d = nc, es, {}

    def get(self, role, idx):
        k = (role, idx)
        if k not in self.d:
            self.d[k] = [self.es.enter_context(self.nc.semaphore(f"sem_{role}_{idx}")), 0]
        return self.d[k]


class Rot:
    def __init__(self, items):
        self.items = list(items)
        self.i = 0

    def next(self):
        r = self.items[self.i % len(self.items)]
        self.i += 1
        return r


def bc(ap, shape):
    return ap.to_broadcast(list(shape))


def build(S, LC, NL, dbg=False, phases=None):
    TT = S + LC
    NT = TT // 128
    NTX = S // 128
    nc = bass.Bass("TRN2", target_bir_lowering=False)
    gstack = ExitStack()
    Prog.sempool = SemPool(nc, gstack)

    def din(name, shape):
        return nc.dram_tensor(name, list(shape), F32, kind="ExternalInput").ap()

    x = din("x", [S, D]); cvec = din("c", [D]); ctx = din("ctx", [LC, D]); c_ctx = din("c_ctx", [D])
    norm_g = din("norm_g", [NL, D]); w_ada = din("w_ada", [NL, D, 3 * D]); b_ada = din("b_ada", [NL, 3 * D])
    w_in = din("w_in", [NL, D, IN_COLS])
    mla_q_norm = din("mla_q_norm", [NL, 512]); mla_w_uq = din("mla_w_uq", [NL, 512, 1152])
    mla_kv_norm = din("mla_kv_norm", [NL, 256]); mla_w_ukv = din("mla_w_ukv", [NL, 256, 1536])
    mla_q_gain = din("mla_q_gain", [NL, 192]); mla_k_gain = din("mla_k_gain", [NL, 192])
    swa_q_gain = din("swa_q_gain", [NL, 128]); swa_k_gain = din("swa_k_gain", [NL, 128])
    swa_sink = din("swa_sink", [NL, 6])
    dif_q_gain = din("dif_q_gain", [NL, 64]); dif_k_gain = din("dif_k_gain", [NL, 64])
    dif_lq1 = din("dif_lq1", [NL, 64]); dif_lk1 = din("dif_lk1", [NL, 64])
    dif_lq2 = din("dif_lq2", [NL, 64]); dif_lk2 = din("dif_lk2", [NL, 64])
    dif_out_gain = din("dif_out_gain", [NL, 128]); w_out = din("w_out", [NL, D, D])
    r64c = din("r64c", [S, 32]); r64s = din("r64s", [S, 32])
    r128c = din("r128c", [S, 64]); r128s = din("r128s", [S, 64])
    out = nc.dram_tensor("out", [S, D], F32, kind="ExternalOutput").ap()

    skind = "ExternalOutput" if dbg else "Internal"
    _pid = [0]

    def next_pid():
        _pid[0] += 1
        return f"ph{_pid[0]}"

    def scr(name, shape, dt):
        return nc.dram_tensor(name, list(shape), dt, kind=skind).ap()

    mod = scr("mod", [NL, 2, 3 * D], F32)
    mla_qT = scr("mla_qT", [6, 192, TT], BF16); mla_kT = scr("mla_kT", [6, 192, TT], BF16)
    mla_v = scr("mla_v", [6, TT, 128], BF16)
    swa_qT = scr("swa_qT", [6, 128, TT], BF16); swa_kT = scr("swa_kT", [2, 128, TT], BF16)
    swa_v = scr("swa_v", [2, TT, 128], BF16)
    dif_qT = scr("dif_qT", [4, 128, TT], BF16); dif_kT = scr("dif_kT", [4, 128, TT], BF16)
    dif_v = scr("dif_v", [4, TT, 128], BF16)
    gates = scr("gates", [TT, D], BF16)
    ysc = scr("ysc", [TT, D], BF16)
    hx1 = scr("hx1", [S, D], F32); hc1 = scr("hc1", [LC, D], F32)

    def h_src(l, t):
        if t < NTX:
            base = x if l == 0 else hx1
            return base[t * 128:(t + 1) * 128, :]
        base = ctx if l == 0 else hc1
        return base[(t - NTX) * 128:(t - NTX + 1) * 128, :]

    def h_dst(l, t):
        if t < NTX:
            base = out if l == NL - 1 else hx1
            return base[t * 128:(t + 1) * 128, :]
        return hc1[(t - NTX) * 128:(t - NTX + 1) * 128, :]

    def make_ident(p, sb):
        ident_f = sb("ident_f", [128, 128], F32)
        ident = sb("ident", [128, 128], BF16)
        p.op("pool", lambda e: e.memset(ident_f[:], 1.0), writes=["ident_f"])
        p.op("pool", lambda e: e.affine_select(out=ident_f[:], in_=ident_f[:], pattern=[[-1, 128]],
                                               compare_op=ALU.is_equal, fill=0.0, base=0, channel_multiplier=1),
             reads=["ident_f"], writes=["ident_f"])
        p.op("dve", lambda e: e.tensor_copy(out=ident[:], in_=ident_f[:]), reads=["ident_f"], writes=["ident"])
        return ident

    def rstd_from_ss(p, ss_ap, ss_res, n, scratch_ap=None):
        p.op("act", lambda e: e.activation(out=ss_ap, in_=ss_ap, func=AF.Sqrt, scale=1.0 / n, bias=EPS),
             reads=[ss_res], writes=[ss_res])
        p.op("dve", lambda e: e.reciprocal(out=ss_ap, in_=ss_ap), reads=[ss_res], writes=[ss_res])

    def phase_ada(l):
        with ExitStack() as es:
            pid = next_pid()
            sb = lambda n, s, d: es.enter_context(nc.sbuf_tensor(f"{pid}_{n}", list(s), d))
            ps = lambda n, s, d: es.enter_context(nc.psum_tensor(f"{pid}_{n}", list(s), d))
            p = Prog(nc)
            cT_f = sb("cT_f", [128, KD, 2], F32)
            cT_bf = sb("cT_bf", [128, KD, 2], BF16)
            wch = [sb(f"wch{i}", [128, KD, 512], BF16) for i in range(3)]
            m_sb = sb("m_sb", [2, 3 * D], F32)
            b_sb = sb("b_sb", [2, 3 * D], F32)
            ng = sb("ng", [2, D], F32)
            acc = [ps(f"acc{i}", [128, 512], F32) for i in range(2)]
            p.dma("sp", cT_f[:, :, 0], cvec.rearrange("(k p) -> p k", p=128), writes=["cT_f0"], key="c0",
                  allow_slow_non_contiguous=True)
            p.dma("sp", cT_f[:, :, 1], c_ctx.rearrange("(k p) -> p k", p=128), writes=["cT_f1"], key="c1",
                  allow_slow_non_contiguous=True)
            p.dma("sp", b_sb[:], b_ada[l].partition_broadcast(2), writes=["b_sb"], key="c2")
            p.dma("sp", ng[:], norm_g[l].partition_broadcast(2), writes=["ng"], key="c3")
            p.op("act", lambda e: e.activation(out=cT_bf[:], in_=cT_f[:], func=AF.Silu),
                 reads=["cT_f0", "cT_f1"], writes=["cT_bf"])
            def ld_w(cg):
                wb = cg % 3
                p.dma("pool", wch[wb][:], w_ada[l][:, cg * 512:(cg + 1) * 512].rearrange("(k p) n -> p k n", p=128),
                      writes=[f"wch{wb}"], key=f"w{wb}")
            ld_w(0); ld_w(1)
            for cg in range(12):
                wb = cg % 3
                if cg + 2 < 12:
                    ld_w(cg + 2)
                ab = cg % 2
                for k in range(KD):
                    p.op("pe", lambda e, k=k, wb=wb, ab=ab: e.matmul(acc[ab][0:2, :], lhsT=cT_bf[:, k, :], rhs=wch[wb][:, k, :],
                                                                     start=(k == 0), stop=(k == KD - 1)),
                         reads=["cT_bf", f"wch{wb}"], writes=[f"acc{ab}"])
                p.op("dve", lambda e, cg=cg, ab=ab: e.tensor_tensor(out=m_sb[:, cg * 512:(cg + 1) * 512], in0=acc[ab][0:2, :],
                                                                    in1=b_sb[:, cg * 512:(cg + 1) * 512], op=ALU.add),
                     reads=[f"acc{ab}", "b_sb"], writes=["m_sb"])
            p.op("dve", lambda e: e.scalar_tensor_tensor(out=m_sb[:, D:2 * D], in0=m_sb[:, D:2 * D], scalar=1.0, in1=ng[:],
                                                         op0=ALU.add, op1=ALU.mult),
                 reads=["m_sb", "ng"], writes=["m_sb"])
            p.dma("sp", mod[l], m_sb[:], reads=["m_sb"], key="st")
            p.emit()

    def phase_p1(l):
        need_ctx_q = l < NL - 1
        with ExitStack() as es:
            pid = next_pid()
            sb = lambda n, s, d: es.enter_context(nc.sbuf_tensor(f"{pid}_{n}", list(s), d))
            ps = lambda n, s, d: es.enter_context(nc.psum_tensor(f"{pid}_{n}", list(s), d))
            p = Prog(nc)
            ident = make_ident(p, sb)
            SUP = 1024 if S >= 1024 else S
            NSUP = SUP // 128
            nxT = sb("nxT", [128, KD, SUP], BF16)
            wch = [sb(f"wch{i}", [128, KD, 512], BF16) for i in range(2)]
            hb = [sb(f"hb{i}", [128, D], F32) for i in range(2)]
            junk = sb("junk", [128, D], BF16)
            junkA = [sb(f"junkA{i}", [128, 512], BF16) for i in range(3)]
            junkR = Rot(range(3))
            nxb = [sb(f"nxb{i}", [128, D], BF16) for i in range(2)]
            Gt = sb("Gt", [128, D], F32); SHt = sb("SHt", [128, D], F32)
            wuq = sb("wuq", [128, 4, 1152], BF16); wukv = sb("wukv", [128, 2, 1536], BF16)
            g_qn = sb("g_qn", [128, 512], F32); g_kvn = sb("g_kvn", [128, 256], F32)
            g_mq = sb("g_mq", [128, 192], F32); g_mk = sb("g_mk", [128, 192], F32)
            g_sq = sb("g_sq", [128, 128], F32); g_sk = sb("g_sk", [128, 128], F32)
            g_dq = sb("g_dq", [128, 64], F32); g_dk = sb("g_dk", [128, 64], F32)
            rc64 = sb("rc64", [128, NSUP, 32], F32); rs64 = sb("rs64", [128, NSUP, 32], F32)
            rc128 = sb("rc128", [128, NSUP, 64], F32); rs128 = sb("rs128", [128, NSUP, 64], F32)
            NSQ = 2
            sq = [sb(f"sq{i}", [128, 1152], F32) for i in range(NSQ)]
            t2 = [sb(f"t2{i}", [128, 1152], F32) for i in range(NSQ)]
            ra = [sb(f"ra{i}", [128, 512], F32) for i in range(2)]
            rb = [sb(f"rb{i}", [128, 512], F32) for i in range(2)]
            NSM = 12
            small = [sb(f"small{i}", [128, 8], F32) for i in range(NSM)]
            ssh = sb("ssh", [128, 2], F32)
            NA = 4
            cqn = [sb(f"cqn{i}", [128, 512], BF16) for i in range(NA)]
            ckvn = [sb(f"ckvn{i}", [128, 256], BF16) for i in range(NA)]
            krf = [sb(f"krf{i}", [128, 64], F32) for i in range(NA)]
            krr = [sb(f"krr{i}", [128, 64], F32) for i in range(2)]
            NT_ = 3
            cqT = [sb(f"cqT{i}", [128, 4, 128], BF16) for i in range(NT_)]
            ckvT = [sb(f"ckvT{i}", [128, 2, 128], BF16) for i in range(NT_)]
            NOL, NOS = 6, 8
            obfL = [sb(f"obfL{i}", [128, 1152], BF16) for i in range(NOL)]
            obfS = [sb(f"obfS{i}", [128, 512], BF16) for i in range(NOS)]
            vst = [sb(f"vst{i}", [128, 768], BF16) for i in range(2)]
            tst = [sb(f"tst{i}", [128, 6, 128], BF16) for i in range(2)]
            tsr = [sb(f"tsr{i}", [128, 6, 128], BF16) for i in range(2)]
            accs = [ps(f"acc{i}", [128, 512], F32) for i in range(2)]
            tps = [ps(f"tp{i}", [128, 1024], BF16) for i in range(2)]
            big = ps("big", [128, 1536], F32)
            sqR, t2R, raR, rbR, smallR = Rot(range(NSQ)), Rot(range(NSQ)), Rot(range(2)), Rot(range(2)), Rot(range(NSM))
            vstR, tstR, tsrR, tpR, accR, hbR, nxbR, krrR = (Rot(range(2)) for _ in range(8))
            cqnR, ckvnR, cqTR, ckvTR, obfLR, obfSR = Rot(range(NA)), Rot(range(NA)), Rot(range(NT_)), Rot(range(NT_)), Rot(range(NOL)), Rot(range(NOS))

            defq = []

            def defer(lag, fn):
                defq.append([lag, fn])

            def tick():
                cur = list(defq)
                del defq[:]
                for d in cur:
                    d[0] -= 1
                    if d[0] <= 0:
                        d[1]()
                    else:
                        defq.append(d)

            def flush():
                while defq:
                    tick()

            import os as _os
            LAG_B1, LAG_B2, LAG_C = [int(v) for v in _os.environ.get("P1LAGS", "1,1,2").split(",")]

            def ld_bc(t, name, src, n):
                p.dma("sp", t[:], src.partition_broadcast(128), writes=[name], key="const")
            ld_bc(g_qn, "g_qn", mla_q_norm[l], 512); ld_bc(g_kvn, "g_kvn", mla_kv_norm[l], 256)
            ld_bc(g_mq, "g_mq", mla_q_gain[l], 192); ld_bc(g_mk, "g_mk", mla_k_gain[l], 192)
            ld_bc(g_sq, "g_sq", swa_q_gain[l], 128); ld_bc(g_sk, "g_sk", swa_k_gain[l], 128)
            ld_bc(g_dq, "g_dq", dif_q_gain[l], 64); ld_bc(g_dk, "g_dk", dif_k_gain[l], 64)
            p.seal("const")
            p.dma("pool", wuq[:], mla_w_uq[l].rearrange("(k p) n -> p k n", p=128), writes=["wuq"], key="wuq")
            p.dma("pool", wukv[:], mla_w_ukv[l].rearrange("(k p) n -> p k n", p=128), writes=["wukv"], key="wukv")

            def load_mod(row):
                p.dma("sp", SHt[:], mod[l, row, 0:D].partition_broadcast(128), writes=["SHt"], key="modsh")
                p.dma("sp", Gt[:], mod[l, row, D:2 * D].partition_broadcast(128), writes=["Gt"], key="modg")

            def transposes(src_fn, n, width, dst_fn, dst_res, src_res, copy_eng="act"):
                i = 0
                while i < n:
                    m = min(4, n - i)
                    tb = tpR.next()
                    for j in range(m):
                        p.op("pe", lambda e, i=i, j=j, tb=tb: e.transpose(out=tps[tb][0:width, j * 128:(j + 1) * 128],
                                                                          in_=src_fn(i + j), identity=ident[:]),
                             reads=[src_res, "ident"], writes=[f"tp{tb}"])
                    src_ps = lambda tb=tb, m=m: tps[tb][0:width, 0:m * 128].rearrange("p (a b) -> p a b", a=m)
                    if copy_eng == "act":
                        p.op("act", lambda e, i=i, m=m, src_ps=src_ps: e.copy(out=dst_fn(i, m), in_=src_ps()),
                             reads=[f"tp{tb}"], writes=[dst_res])
                    else:
                        p.op("dve", lambda e, i=i, m=m, src_ps=src_ps: e.tensor_copy(out=dst_fn(i, m), in_=src_ps()),
                             reads=[f"tp{tb}"], writes=[dst_res])
                    i += m

            def head_norm(src, src_res, H, d, gain_ap, gain_res, rope, out_ap, out_res, is_ctx):
                si, ti, mi = sqR.next(), t2R.next(), smallR.next()
                sqv = sq[si][:, 0:H * d].rearrange("p (h d) -> p h d", h=H)
                t2v = t2[ti][:, 0:H * d].rearrange("p (h d) -> p h d", h=H)
                ssv = small[mi][:, 0:H]
                p.op("act", lambda e: e.activation(out=sqv, in_=src, func=AF.Square), reads=[src_res], writes=[f"sq{si}"])
                p.op("dve", lambda e: e.tensor_reduce(out=ssv, in_=sqv, axis=AX.X, op=ALU.add),
                     reads=[f"sq{si}"], writes=[f"small{mi}"])
                rstd_from_ss(p, ssv, f"small{mi}", d)
                p.op("dve", lambda e: e.tensor_tensor(out=t2v, in0=src, in1=bc(ssv.unsqueeze(2), [128, H, d]), op=ALU.mult),
                     reads=[src_res, f"small{mi}"], writes=[f"t2{ti}"])
                gb = bc(gain_ap.unsqueeze(1), [128, H, d])
                if rope is None or is_ctx:
                    p.op("pool", lambda e: e.tensor_tensor(out=out_ap, in0=t2v, in1=gb, op=ALU.mult),
                         reads=[f"t2{ti}", gain_res], writes=[out_res])
                    return
                half, cs, sn, cres = rope
                r0 = d - 2 * half
                p.op("pool", lambda e: e.tensor_tensor(out=t2v, in0=t2v, in1=gb, op=ALU.mult),
                     reads=[f"t2{ti}", gain_res], writes=[f"t2{ti}"])
                if r0 > 0:
                    p.op("pool", lambda e: e.tensor_copy(out=out_ap[:, :, 0:r0], in_=t2v[:, :, 0:r0]),
                         reads=[f"t2{ti}"], writes=[out_res])
                rope_apply(t2v[:, :, r0:r0 + half], t2v[:, :, r0 + half:d], f"t2{ti}", H, half, cs, sn, cres,
                           out_ap[:, :, r0:r0 + half], out_ap[:, :, r0 + half:d], out_res)

            def rope_apply(x1, x2, xres, H, half, cs, sn, cres, o1, o2, ores):
                ai, bi = raR.next(), rbR.next()
                av = ra[ai][:, 0:H * half].rearrange("p (h d) -> p h d", h=H)
                bv = rb[bi][:, 0:H * half].rearrange("p (h d) -> p h d", h=H)
                cb = bc(cs.unsqueeze(1), [128, H, half]); sbb = bc(sn.unsqueeze(1), [128, H, half])
                p.op("pool", lambda e: e.tensor_tensor(out=av, in0=x1, in1=cb, op=ALU.mult), reads=[xres, *cres], writes=[f"ra{ai}"])
                p.op("pool", lambda e: e.tensor_tensor(out=bv, in0=x2, in1=sbb, op=ALU.mult), reads=[xres, *cres], writes=[f"rb{bi}"])
                p.op("pool", lambda e: e.tensor_tensor(out=o1, in0=av, in1=bv, op=ALU.subtract),
                     reads=[f"ra{ai}", f"rb{bi}"], writes=[ores])
                p.op("pool", lambda e: e.tensor_tensor(out=av, in0=x1, in1=sbb, op=ALU.mult), reads=[xres, *cres], writes=[f"ra{ai}"])
                p.op("pool", lambda e: e.tensor_tensor(out=bv, in0=x2, in1=cb, op=ALU.mult), reads=[xres, *cres], writes=[f"rb{bi}"])
                p.op("pool", lambda e: e.tensor_tensor(out=o2, in0=av, in1=bv, op=ALU.add),
                     reads=[f"ra{ai}", f"rb{bi}"], writes=[ores])

            def store_T(src_bf, src_res, nblk, width, dst_fn, use_rope_stage=False, part_lo=0):
                if use_rope_stage:
                    bi = tsrR.next(); stage = tsr[bi]; sres = f"tsr{bi}"
                else:
                    bi = tstR.next(); stage = tst[bi]; sres = f"tst{bi}"
                transposes(src_bf, nblk, width, lambda i, m: stage[0:width, i:i + m, :], sres, src_res)
                p.dma("sp", dst_fn(), stage[part_lo:width, 0:nblk, :], reads=[sres], key="st_" + sres)

            def obf_get(large):
                R_, bufs, nm = (obfLR, obfL, "obfL") if large else (obfSR, obfS, "obfS")
                oi = R_.next()
                gen = R_.i
                def chk():
                    assert R_.i - gen < len(bufs), "obf ring too shallow for the deferral lag"
                return bufs[oi], f"{nm}{oi}", chk

            def ring_get(R_, bufs, nm):
                i = R_.next()
                gen = R_.i
                def chk():
                    assert R_.i - gen < len(bufs), f"{nm} ring too shallow for the deferral lag"
                return bufs[i], f"{nm}{i}", chk

            def post(name, acc, ares, width, t, is_ctx, tok0, ti):
                tsl = slice(tok0, tok0 + 128)
                rope64 = (32, rc64[:, ti, :], rs64[:, ti, :], ("rc64", "rs64"))
                rope128 = (64, rc128[:, ti, :], rs128[:, ti, :], ("rc128", "rs128"))
                if name == "cq":
                    cq_t, cq_r, cq_chk = ring_get(cqnR, cqn, "cqn")
                    mi = smallR.next(); ssv = small[mi][:, 0:1]; sres = f"small{mi}"
                    ji = junkR.next()
                    p.op("act", lambda e: e.activation(out=junkA[ji][:, 0:512], in_=acc[:, 0:512], func=AF.Square, accum_out=ssv),
                         reads=[ares], writes=[sres, f"junkA{ji}"])
                    rstd_from_ss(p, ssv, sres, 512)
                    p.op("dve", lambda e: e.scalar_tensor_tensor(out=cq_t[:], in0=acc[:, 0:512], scalar=ssv, in1=g_qn[:],
                                                                 op0=ALU.mult, op1=ALU.mult),
                         reads=[ares, sres, "g_qn"], writes=[cq_r])

                    def stage_b1():
                        cq_chk()
                        cT, cT_r, cT_chk = ring_get(cqTR, cqT, "cqT")
                        transposes(lambda i: cq_t[:, i * 128:(i + 1) * 128], 4, 128, lambda i, m: cT[:, i:i + m, :], cT_r, cq_r)

                        def stage_b2():
                            cT_chk()
                            for (c0, cw) in ((0, 512), (512, 512), (1024, 128)):
                                for k in range(4):
                                    p.op("pe", lambda e, c0=c0, cw=cw, k=k: e.matmul(big[:, c0:c0 + cw], lhsT=cT[:, k, :], rhs=wuq[:, k, c0:c0 + cw],
                                                                                     start=(k == 0), stop=(k == 3)),
                                         reads=[cT_r, "wuq"], writes=["big"])
                            ob, ob_r, ob_chk = obf_get(True)
                            ov = ob[:, 0:1152].rearrange("p (h d) -> p h d", h=6)
                            head_norm(big[:, 0:1152].rearrange("p (h d) -> p h d", h=6), "big", 6, 192, g_mq[:], "g_mq", rope64,
                                      ov, ob_r, is_ctx)

                            def stage_c():
                                ob_chk()
                                store_T(lambda i: ov[:, i, 0:128], ob_r, 6, 128,
                                        lambda: mla_qT[:, 0:128, tsl].rearrange("h d t -> d h t"))
                                store_T(lambda i: ov[:, i, 64:192], ob_r, 6, 128,
                                        lambda: mla_qT[:, 128:192, tsl].rearrange("h d t -> d h t"), use_rope_stage=True, part_lo=64)
                            defer(LAG_C, stage_c)
                        defer(LAG_B2, stage_b2)
                    defer(LAG_B1, stage_b1)
                elif name == "ckv":
                    ck_t, ck_r, ck_chk = ring_get(ckvnR, ckvn, "ckvn")
                    kf_i = (ckvnR.i - 1) % NA
                    kf_t, kf_r = krf[kf_i], f"krf{kf_i}"
                    mi = smallR.next(); ssv1 = small[mi][:, 0:1]; ssk = small[mi][:, 1:2]; sres1 = f"small{mi}"
                    ji = junkR.next(); ji2 = junkR.next()
                    p.op("act", lambda e: e.activation(out=junkA[ji][:, 0:256], in_=acc[:, 0:256], func=AF.Square, accum_out=ssv1),
                         reads=[ares], writes=[sres1, f"junkA{ji}"])
                    rstd_from_ss(p, ssv1, sres1, 256)
                    p.op("dve", lambda e: e.scalar_tensor_tensor(out=ck_t[:], in0=acc[:, 0:256], scalar=ssv1, in1=g_kvn[:],
                                                                 op0=ALU.mult, op1=ALU.mult),
                         reads=[ares, sres1, "g_kvn"], writes=[ck_r])
                    p.op("act", lambda e: e.activation(out=junkA[ji2][:, 0:64], in_=acc[:, 256:320], func=AF.Square, accum_out=ssk),
                         reads=[ares], writes=[sres1 + "k", f"junkA{ji2}"])
                    p.op("dve", lambda e: e.tensor_tensor(out=kf_t[:], in0=acc[:, 256:320], in1=g_mk[:, 128:192], op=ALU.mult),
                         reads=[ares, "g_mk"], writes=[kf_r])

                    def stage_b1():
                        ck_chk()
                        cT, cT_r, cT_chk = ring_get(ckvTR, ckvT, "ckvT")
                        transposes(lambda i: ck_t[:, i * 128:(i + 1) * 128], 2, 128, lambda i, m: cT[:, i:i + m, :], cT_r, ck_r)

                        def stage_b2():
                            cT_chk()
                            for c0 in (0, 512, 1024):
                                for k in range(2):
                                    p.op("pe", lambda e, c0=c0, k=k: e.matmul(big[:, c0:c0 + 512], lhsT=cT[:, k, :], rhs=wukv[:, k, c0:c0 + 512],
                                                                              start=(k == 0), stop=(k == 1)),
                                         reads=[cT_r, "wukv"], writes=["big"])
                            kvv = big[:, 0:1536].rearrange("p (h d) -> p h d", h=6)
                            si, m2 = sqR.next(), smallR.next()
                            sqv = sq[si][:, 0:768].rearrange("p (h d) -> p h d", h=6)
                            ssv = small[m2][:, 0:6]
                            p.op("act", lambda e: e.activation(out=sqv, in_=kvv[:, :, 0:128], func=AF.Square), reads=["big"], writes=[f"sq{si}"])
                            p.op("dve", lambda e: e.tensor_reduce(out=ssv, in_=sqv, axis=AX.X, op=ALU.add),
                                 reads=[f"sq{si}"], writes=[f"small{m2}"])
                            p.op("dve", lambda e: e.tensor_scalar(out=ssv, in0=ssv, scalar1=ssk, scalar2=None, op0=ALU.add),
                                 reads=[f"small{m2}", sres1 + "k"], writes=[f"small{m2}"])
                            rstd_from_ss(p, ssv, f"small{m2}", 192)
                            ob, ob_r, ob_chk = obf_get(True)
                            ov = ob[:, 0:1152].rearrange("p (h d) -> p h d", h=6)
                            ti2 = t2R.next()
                            t2v = t2[ti2][:, 0:768].rearrange("p (h d) -> p h d", h=6)
                            p.op("dve", lambda e: e.tensor_tensor(out=t2v, in0=kvv[:, :, 0:128], in1=bc(ssv.unsqueeze(2), [128, 6, 128]), op=ALU.mult),
                                 reads=["big", f"small{m2}"], writes=[f"t2{ti2}"])
                            p.op("pool", lambda e: e.tensor_tensor(out=ov[:, :, 0:128], in0=t2v, in1=bc(g_mk[:, 0:128].unsqueeze(1), [128, 6, 128]),
                                                                   op=ALU.mult),
                                 reads=[f"t2{ti2}", "g_mk"], writes=[ob_r])
                            if is_ctx:
                                krsrc, krres = kf_t, kf_r
                            else:
                                ki = krrR.next()
                                rope_apply(kf_t[:, 0:32].unsqueeze(1), kf_t[:, 32:64].unsqueeze(1), kf_r, 1, 32, rc64[:, ti, :], rs64[:, ti, :], ("rc64", "rs64"),
                                           krr[ki][:, 0:32].unsqueeze(1), krr[ki][:, 32:64].unsqueeze(1), f"krr{ki}")
                                krsrc, krres = krr[ki], f"krr{ki}"
                            p.op("pool", lambda e: e.tensor_tensor(out=ov[:, :, 128:192], in0=bc(krsrc[:].unsqueeze(1), [128, 6, 64]),
                                                                   in1=bc(ssv.unsqueeze(2), [128, 6, 64]), op=ALU.mult),
                                 reads=[krres, f"small{m2}"], writes=[ob_r])
                            vi = vstR.next()
                            vv = vst[vi][:, 0:768].rearrange("p (h d) -> p h d", h=6)
                            p.op("act", lambda e: e.copy(out=vv, in_=kvv[:, :, 128:256]), reads=["big"], writes=[f"vst{vi}"])
                            p.dma("sp", mla_v[:, tsl, :].rearrange("h t d -> t h d"), vv, reads=[f"vst{vi}"], key=f"st_vst{vi}")

                            def stage_c():
                                ob_chk()
                                store_T(lambda i: ov[:, i, 0:128], ob_r, 6, 128,
                                        lambda: mla_kT[:, 0:128, tsl].rearrange("h d t -> d h t"))
                                store_T(lambda i: ov[:, i, 64:192], ob_r, 6, 128,
                                        lambda: mla_kT[:, 128:192, tsl].rearrange("h d t -> d h t"), use_rope_stage=True, part_lo=64)
                            defer(LAG_C, stage_c)
                        defer(LAG_B2, stage_b2)
                    defer(LAG_B1, stage_b1)
                elif name in ("mgate0", "mgate1", "sgate0", "sgate1", "dgate"):
                    col = {"mgate0": 0, "mgate1": 384, "sgate0": 768, "sgate1": 1152, "dgate": 1536}[name]
                    vi = vstR.next()
                    p.op("act", lambda e: e.activation(out=vst[vi][:, 0:width], in_=acc[:, 0:width], func=AF.Silu),
                         reads=[ares], writes=[f"vst{vi}"])
                    p.dma("sp", gates[tsl, col:col + width], vst[vi][:, 0:width], reads=[f"vst{vi}"], key=f"st_vst{vi}")
                elif name in ("swaq0", "swaq1"):
                    h0 = 0 if name == "swaq0" else 3
                    ob, ob_r, ob_chk = obf_get(False)
                    ov = ob[:, 0:384].rearrange("p (h d) -> p h d", h=3)
                    head_norm(acc[:, 0:384].rearrange("p (h d) -> p h d", h=3), ares, 3, 128, g_sq[:], "g_sq", rope128,
                              ov, ob_r, is_ctx)

                    def stage_c():
                        ob_chk()
                        store_T(lambda i: ov[:, i, :], ob_r, 3, 128,
                                lambda: swa_qT[h0:h0 + 3, :, tsl].rearrange("h d t -> d h t"))
                    defer(LAG_C, stage_c)
                elif name == "swakv":
                    ob, ob_r, ob_chk = obf_get(False)
                    ov = ob[:, 0:256].rearrange("p (h d) -> p h d", h=2)
                    head_norm(acc[:, 0:256].rearrange("p (h d) -> p h d", h=2), ares, 2, 128, g_sk[:], "g_sk", rope128,
                              ov, ob_r, is_ctx)
                    vi = vstR.next()
                    vv = vst[vi][:, 0:256].rearrange("p (h d) -> p h d", h=2)
                    p.op("act", lambda e: e.copy(out=vv, in_=acc[:, 256:512].rearrange("p (h d) -> p h d", h=2)),
                         reads=[ares], writes=[f"vst{vi}"])
                    p.dma("sp", swa_v[:, tsl, :].rearrange("h t d -> t h d"), vv, reads=[f"vst{vi}"], key=f"st_vst{vi}")

                    def stage_c():
                        ob_chk()
                        store_T(lambda i: ov[:, i, :], ob_r, 2, 128,
                                lambda: swa_kT[:, :, tsl].rearrange("h d t -> d h t"))
                    defer(LAG_C, stage_c)
                elif name in ("difq", "difk"):
                    ob, ob_r, ob_chk = obf_get(False)
                    ov = ob[:, 0:512].rearrange("p (h d) -> p h d", h=8)
                    gap, gres = (g_dq, "g_dq") if name == "difq" else (g_dk, "g_dk")
                    head_norm(acc[:, 0:512].rearrange("p (h d) -> p h d", h=8), ares, 8, 64, gap[:], gres, rope64,
                              ov, ob_r, is_ctx)
                    dstT = dif_qT if name == "difq" else dif_kT

                    def stage_c():
                        ob_chk()
                        store_T(lambda i: ob[:, i * 128:(i + 1) * 128], ob_r, 4, 128,
                                lambda: dstT[:, :, tsl].rearrange("h d t -> d h t"))
                    defer(LAG_C, stage_c)
                elif name == "difv":
                    vi = vstR.next()
                    vv = vst[vi][:, 0:512].rearrange("p (h d) -> p h d", h=4)
                    p.op("act", lambda e: e.copy(out=vv, in_=acc[:, 0:512].rearrange("p (h d) -> p h d", h=4)),
                         reads=[ares], writes=[f"vst{vi}"])
                    p.dma("sp", dif_v[:, tsl, :].rearrange("h t d -> t h d"), vv, reads=[f"vst{vi}"], key=f"st_vst{vi}")
                else:
                    raise ValueError(name)

            sups = [(t0, min(NSUP, NTX - t0), False) for t0 in range(0, NTX, NSUP)]
            sups.append((NTX, NT - NTX, True))
            skip_ctx = ("cq", "swaq0", "swaq1", "difq")
            work = []
            for si_, (t0, ntile, is_ctx) in enumerate(sups):
                for gi, (name, c0, width) in enumerate(GROUPS):
                    if is_ctx and (not need_ctx_q) and name in skip_ctx:
                        continue
                    work.append((si_, name, c0, width))

            def issue_w(i):
                if i >= len(work):
                    return
                (_, name, c0, width) = work[i]
                wb = i % 2
                p.dma("pool", wch[wb][:, :, 0:width], w_in[l][:, c0:c0 + width].rearrange("(k p) n -> p k n", p=128),
                      writes=[f"wch{wb}"], key=f"w{wb}")

            def issue_h(t):
                hi = hbR.next()
                p.dma("sp", hb[hi][:], h_src(l, t), writes=[f"hb{hi}"], key=f"hb{hi}")
                return hi

            issue_w(0)
            wi = 0
            cur_mod = None
            pre_h = None
            for si_, (t0, ntile, is_ctx) in enumerate(sups):
                row = 1 if is_ctx else 0
                flush()
                if cur_mod != row:
                    load_mod(row)
                    cur_mod = row
                if not is_ctx:
                    tr = slice(t0 * 128, (t0 + ntile) * 128)
                    p.dma("sp", rc64[:, 0:ntile, :], r64c[tr, :].rearrange("(t p) d -> p t d", p=128), writes=["rc64"], key="rope")
                    p.dma("sp", rs64[:, 0:ntile, :], r64s[tr, :].rearrange("(t p) d -> p t d", p=128), writes=["rs64"], key="rope")
                    p.dma("sp", rc128[:, 0:ntile, :], r128c[tr, :].rearrange("(t p) d -> p t d", p=128), writes=["rc128"], key="rope")
                    p.dma("sp", rs128[:, 0:ntile, :], r128s[tr, :].rearrange("(t p) d -> p t d", p=128), writes=["rs128"], key="rope")
                    p.seal("rope")
                nxt_h = pre_h if pre_h is not None else issue_h(t0)
                pre_h = None
                for ti in range(ntile):
                    t = t0 + ti
                    hi = nxt_h
                    if ti + 1 < ntile:
                        nxt_h = issue_h(t + 1)
                    p.op("act", lambda e, hi=hi: e.activation(out=junk[:], in_=hb[hi][:], func=AF.Square, accum_out=ssh[:, 0:1]),
                         reads=[f"hb{hi}"], writes=["ssh", "junk"])
                    rstd_from_ss(p, ssh[:, 0:1], "ssh", D)
                    p.op("dve", lambda e, hi=hi: e.scalar_tensor_tensor(out=hb[hi][:], in0=hb[hi][:], scalar=ssh[:, 0:1], in1=Gt[:],
                                                                        op0=ALU.mult, op1=ALU.mult),
                         reads=[f"hb{hi}", "ssh", "Gt"], writes=[f"hb{hi}"])
                    ni = nxbR.next()
                    p.op("pool", lambda e, ni=ni, hi=hi: e.tensor_tensor(out=nxb[ni][:], in0=hb[hi][:], in1=SHt[:], op=ALU.add),
                         reads=[f"hb{hi}", "SHt"], writes=[f"nxb{ni}"])
                    transposes(lambda i, ni=ni: nxb[ni][:, i * 128:(i + 1) * 128], KD, 128,
                               lambda i, m, ti=ti: nxT[:, i:i + m, ti * 128:(ti + 1) * 128], "nxT", f"nxb{ni}",
                               copy_eng="dve")
                if si_ + 1 < len(sups):
                    pre_h = issue_h(sups[si_ + 1][0])
                while wi < len(work) and work[wi][0] == si_:
                    (_, name, c0, width) = work[wi]
                    wb = wi % 2
                    issue_w(wi + 1)
                    for ti in range(ntile):
                        t = t0 + ti
                        ai = accR.next()
                        for k in range(KD):
                            p.op("pe", lambda e, ai=ai, k=k, ti=ti, wb=wb, width=width: e.matmul(
                                accs[ai][:, 0:width], lhsT=nxT[:, k, ti * 128:(ti + 1) * 128], rhs=wch[wb][:, k, 0:width],
                                start=(k == 0), stop=(k == KD - 1)),
                                 reads=["nxT", f"wch{wb}"], writes=[f"acc{ai}"])
                        post(name, accs[ai], f"acc{ai}", width, t, is_ctx, t * 128, ti)
                        tick()
                    wi += 1
            flush()
            p.emit()

    def q_ranges(l, qg):
        r = []
        for q0 in range(0, S, qg):
            r.append((q0, min(qg, S - q0), list(range(NT)), False))
        if l < NL - 1:
            for q0 in range(S, TT, qg):
                r.append((q0, min(qg, TT - q0), list(range(NTX, NT)), True))
        return r

    def phase_mla(l):
        TQ = TT if l < NL - 1 else S
        NTQ = TQ // 128
        scale = 192 ** -0.5
        with ExitStack() as es:
            pid = next_pid()
            sb = lambda n, s, d: es.enter_context(nc.sbuf_tensor(f"{pid}_{n}", list(s), d))
            ps = lambda n, s, d: es.enter_context(nc.psum_tensor(f"{pid}_{n}", list(s), d))
            p = Prog(nc)
            NB = 2
            qTn = [sb(f"qTn{i}", [128, TQ], BF16) for i in range(NB)]
            qTr = [sb(f"qTr{i}", [128, TQ], BF16) for i in range(NB)]
            kTn = [sb(f"kTn{i}", [128, TT], BF16) for i in range(NB)]
            kTr = [sb(f"kTr{i}", [128, TT], BF16) for i in range(NB)]
            V1 = [sb(f"V1{i}", [128, NT, 129], BF16) for i in range(NB)]
            sg = [sb(f"sg{i}", [128, NTQ, 128], BF16) for i in range(NB)]
            yb = [sb(f"yb{i}", [128, NTQ, 128], BF16) for i in range(NB)]
            pT = [sb(f"pT{i}", [128, 512], BF16) for i in range(4)]
            rs = sb("rs", [128, 4], F32)
            sT = [ps(f"sT{i}", [128, 512], F32) for i in range(3)]
            O = [ps(f"O{i}", [128, 512], F32) for i in range(4)]
            for i in range(NB):
                p.op("pool", lambda e, i=i: e.memset(V1[i][:, :, 128:129], 1.0), writes=[f"V1{i}"])
                p.op("pool", lambda e, i=i: e.memset(qTr[i][64:128, :], 0.0), writes=[f"qTr{i}"])
                p.op("pool", lambda e, i=i: e.memset(kTr[i][64:128, :], 0.0), writes=[f"kTr{i}"])
            sTR, pTR = Rot(range(3)), Rot(range(4))
            def load_head(h):
                b = h % NB
                p.dma("sp", qTn[b][:], mla_qT[h, 0:128, 0:TQ], writes=[f"qTn{b}"], key=f"qTn{b}")
                p.dma("sp", qTr[b][0:64, :], mla_qT[h, 128:192, 0:TQ], writes=[f"qTr{b}"], key=f"qTr{b}")
                p.dma("sp", kTn[b][:], mla_kT[h, 0:128, :], writes=[f"kTn{b}"], key=f"kTn{b}")
                p.dma("sp", kTr[b][0:64, :], mla_kT[h, 128:192, :], writes=[f"kTr{b}"], key=f"kTr{b}")
                p.dma("sp", V1[b][:, :, 0:128], mla_v[h].rearrange("(t p) d -> p t d", p=128), writes=[f"V1{b}"], key=f"V1{b}")
                p.dma("sp", sg[b][:], gates[0:TQ, h * 128:(h + 1) * 128].rearrange("(t p) d -> p t d", p=128),
                      writes=[f"sg{b}"], key=f"sg{b}")
            load_head(0)
            for h in range(6):
                b = h % NB
                if h + 1 < 6:
                    load_head(h + 1)
                for (q0, qn, chunks, _) in q_ranges(l, 512):
                    nq = qn // 128
                    pend = []

                    def do_pv(item, b=b, nq=nq, chunks=chunks):
                        (ci, kc, pi) = item
                        for qi in range(nq):
                            p.op("pe", lambda e, qi=qi, kc=kc, pi=pi, ci=ci: e.matmul(
                                O[qi][:, 0:129], lhsT=pT[pi][:, qi * 128:(qi + 1) * 128], rhs=V1[b][:, kc, :],
                                start=(ci == 0), stop=(ci == len(chunks) - 1)),
                                 reads=[f"pT{pi}", f"V1{b}"], writes=[f"O{qi}"])
                    for ci, kc in enumerate(chunks):
                        si = sTR.next()
                        p.op("pe", lambda e, si=si, kc=kc, b=b, q0=q0, qn=qn: e.matmul(
                            sT[si][:, 0:qn], lhsT=kTn[b][:, kc * 128:(kc + 1) * 128], rhs=qTn[b][:, q0:q0 + qn], start=True, stop=False),
                             reads=[f"kTn{b}", f"qTn{b}"], writes=[f"sT{si}"])
                        p.op("pe", lambda e, si=si, kc=kc, b=b, q0=q0, qn=qn: e.matmul(
                            sT[si][:, 0:qn], lhsT=kTr[b][:, kc * 128:(kc + 1) * 128], rhs=qTr[b][:, q0:q0 + qn], start=False, stop=True),
                             reads=[f"kTr{b}", f"qTr{b}"], writes=[f"sT{si}"])
                        pi = pTR.next()
                        p.op("act", lambda e, si=si, pi=pi, qn=qn: e.activation(out=pT[pi][:, 0:qn], in_=sT[si][:, 0:qn], func=AF.Exp, scale=scale),
                             reads=[f"sT{si}"], writes=[f"pT{pi}"])
                        pend.append((ci, kc, pi))
                        if len(pend) > 2:
                            do_pv(pend.pop(0))
                    while pend:
                        do_pv(pend.pop(0))
                    for qi in range(nq):
                        tq = q0 // 128 + qi
                        p.op("dve", lambda e, qi=qi: e.reciprocal(out=rs[:, qi:qi + 1], in_=O[qi][:, 128:129]),
                             reads=[f"O{qi}"], writes=[f"rs{qi}"])
                        p.op("dve", lambda e, qi=qi, tq=tq, b=b: e.scalar_tensor_tensor(
                            out=yb[b][:, tq, :], in0=O[qi][:, 0:128], scalar=rs[:, qi:qi + 1], in1=sg[b][:, tq, :],
                            op0=ALU.mult, op1=ALU.mult),
                             reads=[f"O{qi}", f"rs{qi}", f"sg{b}"], writes=[f"yb{b}"])
                p.dma("sp", ysc[0:TQ, h * 128:(h + 1) * 128].rearrange("(t p) d -> p t d", p=128), yb[b][:],
                      reads=[f"yb{b}"], key=f"st_yb{b}")
            p.emit()

    def phase_swa(l):
        TQ = TT if l < NL - 1 else S
        NTQ = TQ // 128
        scale = 128 ** -0.5
        with ExitStack() as es:
            pid = next_pid()
            sb = lambda n, s, d: es.enter_context(nc.sbuf_tensor(f"{pid}_{n}", list(s), d))
            ps = lambda n, s, d: es.enter_context(nc.psum_tensor(f"{pid}_{n}", list(s), d))
            p = Prog(nc)
            qT3 = sb("qT3", [128, 3, TQ], BF16)
            kT = sb("kT", [128, TT], BF16)
            V1 = sb("V1", [128, NT, 129], BF16)
            sg = sb("sg", [128, NTQ, 384], BF16)
            yb = sb("yb", [128, NTQ, 384], BF16)
            pT = [sb(f"pT{i}", [128, 3, 128], BF16) for i in range(3)]
            mk_f = sb("mk_f", [128, 128], F32)
            mprev = sb("mprev", [128, 128], BF16); mnext = sb("mnext", [128, 128], BF16)
            esink = sb("esink", [128, 6], F32)
            rs = sb("rs", [128, 4], F32)
            sT = [ps(f"sT{i}", [128, 512], F32) for i in range(2)]
            O = [ps(f"O{i}", [128, 512], F32) for i in range(3)]
            p.op("pool", lambda e: e.memset(V1[:, :, 128:129], 1.0), writes=["V1"])
            p.op("pool", lambda e: e.memset(mk_f[:], 1.0), writes=["mk_f"])
            p.op("pool", lambda e: e.affine_select(out=mk_f[:], in_=mk_f[:], pattern=[[-1, 128]], compare_op=ALU.is_ge,
                                                   fill=0.0, base=0, channel_multiplier=1), reads=["mk_f"], writes=["mk_f"])
            p.op("dve", lambda e: e.tensor_copy(out=mprev[:], in_=mk_f[:]), reads=["mk_f"], writes=["mprev"])
            p.op("pool", lambda e: e.memset(mk_f[:], 1.0), reads=["mk_f"], writes=["mk_f"])
            p.op("pool", lambda e: e.affine_select(out=mk_f[:], in_=mk_f[:], pattern=[[1, 128]], compare_op=ALU.is_ge,
                                                   fill=0.0, base=0, channel_multiplier=-1), reads=["mk_f"], writes=["mk_f"])
            p.op("dve", lambda e: e.tensor_copy(out=mnext[:], in_=mk_f[:]), reads=["mk_f"], writes=["mnext"])
            p.dma("sp", esink[:], swa_sink[l].partition_broadcast(128), writes=["esink"], key="sink")
            p.op("act", lambda e: e.activation(out=esink[:], in_=esink[:], func=AF.Exp), reads=["esink"], writes=["esink"])
            sTR, pTR = Rot(range(2)), Rot(range(3))
            for g in range(2):
                p.dma("sp", qT3[:], swa_qT[3 * g:3 * g + 3, :, 0:TQ].rearrange("h d t -> d h t"), writes=["qT3"], key="qT3")
                p.dma("sp", kT[:], swa_kT[g], writes=["kT"], key="kT")
                p.dma("sp", V1[:, :, 0:128], swa_v[g].rearrange("(t p) d -> p t d", p=128), writes=["V1"], key="V1")
                p.dma("sp", sg[:], gates[0:TQ, 768 + g * 384:768 + (g + 1) * 384].rearrange("(t p) d -> p t d", p=128),
                      writes=["sg"], key="sg")
                for n in range(NTQ):
                    if n < NTX:
                        chunks = []
                        if n > 0:
                            chunks.append((n - 1, mprev, "mprev"))
                        chunks.append((n, None, None))
                        if n < NTX - 1:
                            chunks.append((n + 1, mnext, "mnext"))
                        chunks += [(c, None, None) for c in range(NTX, NT)]
                    else:
                        chunks = [(c, None, None) for c in range(NTX, NT)]
                    pend = None

                    def do_pv(item, chunks=chunks):
                        (ci, kc, pi) = item
                        for j in range(3):
                            p.op("pe", lambda e, j=j, kc=kc, pi=pi, ci=ci: e.matmul(
                                O[j][:, 0:129], lhsT=pT[pi][:, j, :], rhs=V1[:, kc, :],
                                start=(ci == 0), stop=(ci == len(chunks) - 1)),
                                 reads=[f"pT{pi}", "V1"], writes=[f"O{j}"])
                    for ci, (kc, mk, mres) in enumerate(chunks):
                        si = sTR.next()
                        p.op("pe", lambda e, si=si, kc=kc, n=n: e.matmul(
                            sT[si][:, 0:384].rearrange("p (h q) -> p h q", h=3), lhsT=kT[:, kc * 128:(kc + 1) * 128],
                            rhs=qT3[:, :, n * 128:(n + 1) * 128], start=True, stop=True),
                             reads=["kT", "qT3"], writes=[f"sT{si}"])
                        pi = pTR.next()
                        p.op("act", lambda e, si=si, pi=pi: e.activation(out=pT[pi][:], in_=sT[si][:, 0:384].rearrange("p (h q) -> p h q", h=3),
                                                                         func=AF.Exp, scale=scale),
                             reads=[f"sT{si}"], writes=[f"pT{pi}"])
                        if mk is not None:
                            p.op("pool", lambda e, pi=pi, mk=mk: e.tensor_tensor(out=pT[pi][:], in0=pT[pi][:],
                                                                                 in1=bc(mk[:].unsqueeze(1), [128, 3, 128]), op=ALU.mult),
                                 reads=[f"pT{pi}", mres], writes=[f"pT{pi}"])
                        if pend is not None:
                            do_pv(pend)
                        pend = (ci, kc, pi)
                    do_pv(pend)
                    for j in range(3):
                        hq = 3 * g + j
                        p.op("dve", lambda e, j=j, hq=hq: e.tensor_tensor(out=rs[:, j:j + 1], in0=O[j][:, 128:129], in1=esink[:, hq:hq + 1], op=ALU.add),
                             reads=[f"O{j}", "esink"], writes=[f"rs{j}"])
                        p.op("dve", lambda e, j=j: e.reciprocal(out=rs[:, j:j + 1], in_=rs[:, j:j + 1]), reads=[f"rs{j}"], writes=[f"rs{j}"])
                        p.op("dve", lambda e, j=j, n=n: e.scalar_tensor_tensor(
                            out=yb[:, n, j * 128:(j + 1) * 128], in0=O[j][:, 0:128], scalar=rs[:, j:j + 1], in1=sg[:, n, j * 128:(j + 1) * 128],
                            op0=ALU.mult, op1=ALU.mult),
                             reads=[f"O{j}", f"rs{j}", "sg"], writes=["yb"])
                p.dma("sp", ysc[0:TQ, 768 + g * 384:768 + (g + 1) * 384].rearrange("(t p) d -> p t d", p=128), yb[:],
                      reads=["yb"], key="st_yb")
            p.emit()

    def phase_dif(l):
        TQ = TT if l < NL - 1 else S
        NTQ = TQ // 128
        scale = 64 ** -0.5
        lam_init = 0.8 - 0.6 * math.exp(-0.3 * l)
        with ExitStack() as es:
            pid = next_pid()
            sb = lambda n, s, d: es.enter_context(nc.sbuf_tensor(f"{pid}_{n}", list(s), d))
            ps = lambda n, s, d: es.enter_context(nc.psum_tensor(f"{pid}_{n}", list(s), d))
            p = Prog(nc)
            NB = 2
            qm = [[sb(f"qm{i}_{c}", [128, TQ], BF16) for c in range(2)] for i in range(NB)]
            kT = [sb(f"kT{i}", [128, TT], BF16) for i in range(NB)]
            V1 = [sb(f"V1{i}", [128, NT, 129], BF16) for i in range(NB)]
            sg = [sb(f"sg{i}", [128, NTQ, 128], BF16) for i in range(NB)]
            yb = [sb(f"yb{i}", [128, NTQ, 128], BF16) for i in range(NB)]
            o0 = sb("o0", [128, NTQ, 128], F32)
            NP = 4
            pT = [sb(f"pT{i}", [128, 512], BF16) for i in range(NP)]
            lt = [sb(f"lt{i}", [128, 64], F32) for i in range(4)]
            lam = sb("lam", [128, 4], F32)
            gog = sb("gog", [128, 128], F32)
            rs = sb("rs", [128, 8], F32)
            of = [sb(f"of{i}", [128, 128], F32) for i in range(2)]
            junk = [sb(f"junk{i}", [128, 128], F32) for i in range(2)]
            NS = 3
            sT = [ps(f"sT{i}", [128, 512], F32) for i in range(NS)]
            O = [ps(f"O{i}", [128, 512], F32) for i in range(4)]
            for i in range(NB):
                p.op("pool", lambda e, i=i: e.memset(V1[i][:, :, 128:129], 1.0), writes=[f"V1{i}"])
                p.op("pool", lambda e, i=i: e.memset(qm[i][0][64:128, :], 0.0), writes=[f"qm{i}_0"])
                p.op("pool", lambda e, i=i: e.memset(qm[i][1][0:64, :], 0.0), writes=[f"qm{i}_1"])
            for i, src in enumerate((dif_lq1, dif_lk1, dif_lq2, dif_lk2)):
                p.dma("sp", lt[i][:], src[l].partition_broadcast(128), writes=[f"lt{i}"], key=f"lt{i}")
            p.dma("sp", gog[:], dif_out_gain[l].partition_broadcast(128), writes=["gog"], key="gog")
            p.op("dve", lambda e: e.tensor_tensor(out=lt[0][:], in0=lt[0][:], in1=lt[1][:], op=ALU.mult), reads=["lt0", "lt1"], writes=["lt0"])
            p.op("dve", lambda e: e.tensor_tensor(out=lt[2][:], in0=lt[2][:], in1=lt[3][:], op=ALU.mult), reads=["lt2", "lt3"], writes=["lt2"])
            p.op("dve", lambda e: e.tensor_reduce(out=lam[:, 0:1], in_=lt[0][:], axis=AX.X, op=ALU.add), reads=["lt0"], writes=["lam"])
            p.op("dve", lambda e: e.tensor_reduce(out=lam[:, 1:2], in_=lt[2][:], axis=AX.X, op=ALU.add), reads=["lt2"], writes=["lam"])
            p.op("act", lambda e: e.activation(out=lam[:, 0:2], in_=lam[:, 0:2], func=AF.Exp), reads=["lam"], writes=["lam"])
            p.op("dve", lambda e: e.tensor_tensor(out=lam[:, 2:3], in0=lam[:, 1:2], in1=lam[:, 0:1], op=ALU.subtract), reads=["lam"], writes=["lam"])
            p.op("dve", lambda e: e.tensor_scalar(out=lam[:, 2:3], in0=lam[:, 2:3], scalar1=-lam_init, scalar2=None, op0=ALU.add),
                 reads=["lam"], writes=["lam"])
            p.op("dve", lambda e: e.tensor_scalar(out=gog[:], in0=gog[:], scalar1=(1.0 - lam_init), scalar2=None, op0=ALU.mult),
                 reads=["gog"], writes=["gog"])
            sTR, pTR, ofR, jR = Rot(range(NS)), Rot(range(NP)), Rot(range(2)), Rot(range(2))

            def load_head(h):
                b = h % NB
                p.dma("sp", qm[b][0][0:64, :], dif_qT[h, 0:64, 0:TQ], writes=[f"qm{b}_0"], key=f"qm{b}_0")
                p.dma("sp", qm[b][1][64:128, :], dif_qT[h, 64:128, 0:TQ], writes=[f"qm{b}_1"], key=f"qm{b}_1")
                p.dma("sp", kT[b][:], dif_kT[h], writes=[f"kT{b}"], key=f"kT{b}")
                p.dma("sp", V1[b][:, :, 0:128], dif_v[h].rearrange("(t p) d -> p t d", p=128), writes=[f"V1{b}"], key=f"V1{b}")
                p.dma("sp", sg[b][:], gates[0:TQ, 1536 + h * 128:1536 + (h + 1) * 128].rearrange("(t p) d -> p t d", p=128),
                      writes=[f"sg{b}"], key=f"sg{b}")
            load_head(0)
            for h in range(4):
                b = h % NB
                if h + 1 < 4:
                    load_head(h + 1)
                for c in range(2):
                    psl = slice(c * 64, (c + 1) * 64)
                    for (q0, qn, chunks, _) in q_ranges(l, 512):
                        nq = qn // 128
                        pend = []

                        def do_pv(item, b=b, nq=nq, chunks=chunks):
                            (ci, kc, pi) = item
                            for qi in range(nq):
                                p.op("pe", lambda e, qi=qi, kc=kc, pi=pi, ci=ci: e.matmul(
                                    O[qi][:, 0:129], lhsT=pT[pi][:, qi * 128:(qi + 1) * 128], rhs=V1[b][:, kc, :],
                                    start=(ci == 0), stop=(ci == len(chunks) - 1)),
                                     reads=[f"pT{pi}", f"V1{b}"], writes=[f"O{qi}"])
                        for ci, kc in enumerate(chunks):
                            si = sTR.next()
                            p.op("pe", lambda e, si=si, kc=kc, b=b, q0=q0, qn=qn, c=c: e.matmul(
                                sT[si][:, 0:qn], lhsT=kT[b][:, kc * 128:(kc + 1) * 128],
                                rhs=qm[b][c][:, q0:q0 + qn], start=True, stop=True),
                                 reads=[f"kT{b}", f"qm{b}_{c}"], writes=[f"sT{si}"])
                            pi = pTR.next()
                            p.op("act", lambda e, si=si, pi=pi, qn=qn: e.activation(out=pT[pi][:, 0:qn], in_=sT[si][:, 0:qn], func=AF.Exp, scale=scale),
                                 reads=[f"sT{si}"], writes=[f"pT{pi}"])
                            pend.append((ci, kc, pi))
                            if len(pend) > 2:
                                do_pv(pend.pop(0))
                        while pend:
                            do_pv(pend.pop(0))
                        for qi in range(nq):
                            tq = q0 // 128 + qi
                            r0 = rs[:, qi:qi + 1]; r2 = rs[:, 4 + qi:5 + qi]
                            rres = f"rs{qi}"
                            p.op("dve", lambda e, qi=qi, r0=r0: e.reciprocal(out=r0, in_=O[qi][:, 128:129]), reads=[f"O{qi}"], writes=[rres])
                            if c == 0:
                                p.op("dve", lambda e, qi=qi, r0=r0, tq=tq: e.tensor_scalar(out=o0[:, tq, :], in0=O[qi][:, 0:128], scalar1=r0, scalar2=None, op0=ALU.mult),
                                     reads=[f"O{qi}", rres], writes=[f"o0_{tq}"])
                                continue
                            oi = ofR.next(); ji = jR.next()
                            p.op("dve", lambda e, r0=r0: e.tensor_tensor(out=r0, in0=r0, in1=lam[:, 2:3], op=ALU.mult), reads=[rres, "lam"], writes=[rres])
                            p.op("dve", lambda e, qi=qi, r0=r0, oi=oi, tq=tq: e.scalar_tensor_tensor(out=of[oi][:], in0=O[qi][:, 0:128], scalar=r0, in1=o0[:, tq, :],
                                                                                              op0=ALU.mult, op1=ALU.add),
                                 reads=[f"O{qi}", rres, f"o0_{tq}"], writes=[f"of{oi}"])
                            p.op("act", lambda e, oi=oi, r2=r2, ji=ji: e.activation(out=junk[ji][:], in_=of[oi][:], func=AF.Square, accum_out=r2),
                                 reads=[f"of{oi}"], writes=[f"junk{ji}", rres + "b"])
                            rstd_from_ss(p, r2, rres + "b", 128)
                            p.op("dve", lambda e, oi=oi, r2=r2: e.scalar_tensor_tensor(out=of[oi][:], in0=of[oi][:], scalar=r2, in1=gog[:],
                                                                                    op0=ALU.mult, op1=ALU.mult),
                                 reads=[f"of{oi}", rres + "b", "gog"], writes=[f"of{oi}"])
                            p.op("pool", lambda e, oi=oi, tq=tq, b=b: e.tensor_tensor(out=yb[b][:, tq, :], in0=of[oi][:], in1=sg[b][:, tq, :], op=ALU.mult),
                                 reads=[f"of{oi}", f"sg{b}"], writes=[f"yb{b}"])
                p.dma("sp", ysc[0:TQ, 1536 + h * 128:1536 + (h + 1) * 128].rearrange("(t p) d -> p t d", p=128), yb[b][:],
                      reads=[f"yb{b}"], key=f"st_yb{b}")
            p.emit()

    def phase_p3(l):
        TQ = TT if l < NL - 1 else S
        NTQ = TQ // 128
        with ExitStack() as es:
            pid = next_pid()
            sb = lambda n, s, d: es.enter_context(nc.sbuf_tensor(f"{pid}_{n}", list(s), d))
            ps = lambda n, s, d: es.enter_context(nc.psum_tensor(f"{pid}_{n}", list(s), d))
            p = Prog(nc)
            ident = make_ident(p, sb)
            wo = sb("wo", [128, KD, D], BF16)
            GA = sb("GA", [128, D], F32)
            yt = [sb(f"yt{i}", [128, D], BF16) for i in range(2)]
            yT = [sb(f"yT{i}", [128, KD, 128], BF16) for i in range(2)]
            hb = [sb(f"hb{i}", [128, D], F32) for i in range(2)]
            ob = [sb(f"ob{i}", [128, D], F32) for i in range(2)]
            accs = [ps(f"acc{i}", [128, 512], F32) for i in range(2)]
            tps = [ps(f"tp{i}", [128, 1024], BF16) for i in range(2)]
            for half in range(2):
                p.dma("pool", wo[:, half * 8:(half + 1) * 8, :],
                      w_out[l][half * 1024:(half + 1) * 1024, :].rearrange("(k p) n -> p k n", p=128),
                      writes=[f"wo{half}"], key=f"wo{half}")
            accR, tpR = Rot(range(2)), Rot(range(2))
            cur = [None]

            def stage_T(t):
                b = t % 2
                p.dma("sp", yt[b][:], ysc[t * 128:(t + 1) * 128, :], writes=[f"yt{b}"], key=f"yt{b}")
                p.dma("sp", hb[b][:], h_src(l, t), writes=[f"hb{b}"], key=f"hb{b}")
                for i0 in range(0, KD, 4):
                    tb = tpR.next()
                    for j in range(4):
                        p.op("pe", lambda e, i0=i0, j=j, tb=tb, b=b: e.transpose(out=tps[tb][:, j * 128:(j + 1) * 128],
                                                                                 in_=yt[b][:, (i0 + j) * 128:(i0 + j + 1) * 128], identity=ident[:]),
                             reads=[f"yt{b}", "ident"], writes=[f"tp{tb}"])
                    p.op("act", lambda e, i0=i0, tb=tb, b=b: e.copy(out=yT[b][:, i0:i0 + 4, :],
                                                                    in_=tps[tb][:, 0:512].rearrange("p (a b) -> p a b", a=4)),
                         reads=[f"tp{tb}"], writes=[f"yT{b}"])

            def stage_M(t):
                b = t % 2
                row = 0 if t < NTX else 1
                if cur[0] != row:
                    p.dma("sp", GA[:], mod[l, row, 2 * D:3 * D].partition_broadcast(128), writes=["GA"], key="GA")
                    cur[0] = row
                for cg in range(4):
                    ai = accR.next()
                    for k in range(KD):
                        p.op("pe", lambda e, ai=ai, k=k, b=b, cg=cg: e.matmul(accs[ai][:], lhsT=yT[b][:, k, :], rhs=wo[:, k, cg * 512:(cg + 1) * 512],
                                                                              start=(k == 0), stop=(k == KD - 1)),
                             reads=[f"yT{b}", f"wo{k // 8}"], writes=[f"acc{ai}"])
                    cs = slice(cg * 512, (cg + 1) * 512)
                    p.op("dve", lambda e, ai=ai, b=b, cs=cs: e.tensor_tensor(out=ob[b][:, cs], in0=accs[ai][:], in1=GA[:, cs], op=ALU.mult),
                         reads=[f"acc{ai}", "GA"], writes=[f"ob{b}"])
                    p.op("pool", lambda e, b=b, cs=cs: e.tensor_tensor(out=ob[b][:, cs], in0=ob[b][:, cs], in1=hb[b][:, cs], op=ALU.add),
                         reads=[f"ob{b}", f"hb{b}"], writes=[f"ob{b}"])
                p.dma("sp", h_dst(l, t), ob[b][:], reads=[f"ob{b}"], key=f"st_ob{b}")

            stage_T(0)
            for t in range(NTQ):
                if t + 1 < NTQ:
                    stage_T(t + 1)
                stage_M(t)
            p.emit()

    def want(n):
        return phases is None or n in phases
    for l in range(NL):
        if want("ada"):
            phase_ada(l)
    for l in range(NL):
        if want("p1"):
            phase_p1(l)
        if want("mla"):
            phase_mla(l)
        if want("swa"):
            phase_swa(l)
        if want("dif"):
            phase_dif(l)
        if want("p3"):
            phase_p3(l)
    gstack.close()
    return nc


def rope_tables(n_tokens, rot_dim):
    rows = n_tokens // GRID_W
    t_row = np.repeat(np.arange(rows), GRID_W).astype(np.float32)
    t_col = np.tile(np.arange(GRID_W), rows).astype(np.float32)
    n_freq = rot_dim // 4
    inv = np.power(np.float32(10000.0), -np.arange(n_freq, dtype=np.float32) / np.float32(n_freq)).astype(np.float32)
    ang = np.concatenate([t_row[:, None] * inv, t_col[:, None] * inv], axis=-1).astype(np.float32)
    return np.cos(ang).astype(np.float32), np.sin(ang).astype(np.float32)


_PARAMS = ["norm_g", "w_ada", "b_ada", "w_in", "mla_q_norm", "mla_w_uq", "mla_kv_norm", "mla_w_ukv",
           "mla_q_gain", "mla_k_gain", "swa_q_gain", "swa_k_gain", "swa_sink", "dif_q_gain", "dif_k_gain",
           "dif_lq1", "dif_lk1", "dif_lq2", "dif_lk2", "dif_out_gain", "w_out"]


def make_in_maps(inputs, S, n_cores):
    c64, s64 = rope_tables(S, 64)
    c128, s128 = rope_tables(S, 128)
    shared = {k: np.ascontiguousarray(np.asarray(inputs[k], dtype=np.float32)) for k in _PARAMS}
    shared["c_ctx"] = np.ascontiguousarray(np.asarray(inputs["c_ctx"], dtype=np.float32))
    shared.update(r64c=c64, r64s=s64, r128c=c128, r128s=s128)
    x = np.asarray(inputs["x"], dtype=np.float32); c = np.asarray(inputs["c"], dtype=np.float32)
    ctx = np.asarray(inputs["ctx"], dtype=np.float32)
    maps = []
    for b in range(n_cores):
        m = dict(shared)
        m["x"] = np.ascontiguousarray(x[b]); m["c"] = np.ascontiguousarray(c[b]); m["ctx"] = np.ascontiguousarray(ctx[b])
        maps.append(m)
    return maps


def kernel(**inputs):
    x = inputs["x"]
    B, S, _ = x.shape
    LC = inputs["ctx"].shape[1]
    NL = inputs["w_in"].shape[0]
    nc = build(S, LC, NL)
    in_maps = make_in_maps(inputs, S, B)
    res = run_bass_kernel_spmd(nc, in_maps, core_ids=list(range(B)))
    return np.stack([np.asarray(r["out"], dtype=np.float32) for r in res.results], axis=0)
```
